# Optimizing a Trainium2 kernel written in Bass

```python
import math
import jax, jax.numpy as jnp
from jax import lax
import numpy as np

D_MODEL = 4096
BATCH = 4
SEQ = 2048
DEPTH = 1

D_SSM = D_MODEL // 2
SSM_GROUP = 16
N_SSM_GROUPS = D_SSM // SSM_GROUP
SSM_STATE = 64
D_POOL = D_MODEL // 2
POOL_WINDOWS = (2, 4, 8, 16)
N_POOL_GROUPS = len(POOL_WINDOWS)
POOL_GROUP = D_POOL // N_POOL_GROUPS
D_IN = D_SSM + D_POOL + 2 * D_MODEL
D_FF = 11008
CONV_WIDTH = 3
EPS = 1e-6
MIN_NEG_REAL = -1e-4

kernel_name = "s5_pool_gated_hybrid_block"


def rms_norm(x, g):
    xf = x.astype(jnp.float32)
    r = xf * lax.rsqrt(jnp.mean(xf * xf, axis=-1, keepdims=True) + EPS)
    return (r * g.astype(jnp.float32)).astype(x.dtype)


def s5_branch(u, lam_re, lam_im, log_step, b_re, b_im, c_re, c_im, d, glu_w, glu_b):
    bsz, L, _ = u.shape
    uf = u.astype(jnp.float32)
    ug = uf.reshape(bsz, L, N_SSM_GROUPS, SSM_GROUP)
    lr = jnp.minimum(lam_re.astype(jnp.float32), MIN_NEG_REAL)
    li = lam_im.astype(jnp.float32)
    dt = jnp.exp(log_step.astype(jnp.float32))[:, None]
    mag = jnp.exp(lr * dt)
    ang = li * dt
    ab_re = mag * jnp.cos(ang)
    ab_im = mag * jnp.sin(ang)
    nr = ab_re - 1.0
    ni = ab_im
    den = lr * lr + li * li
    f_re = (nr * lr + ni * li) / den
    f_im = (ni * lr - nr * li) / den
    br = b_re.astype(jnp.float32)
    bi = b_im.astype(jnp.float32)
    bb_re = f_re[..., None] * br - f_im[..., None] * bi
    bb_im = f_re[..., None] * bi + f_im[..., None] * br
    bu_re = jnp.einsum('blgj,gpj->blgp', ug, bb_re)
    bu_im = jnp.einsum('blgj,gpj->blgp', ug, bb_im)
    a_re = jnp.broadcast_to(ab_re, (1, L) + ab_re.shape)
    a_im = jnp.broadcast_to(ab_im, (1, L) + ab_im.shape)

    def combine(e1, e2):
        a1r, a1i, b1r, b1i = e1
        a2r, a2i, b2r, b2i = e2
        return (a2r * a1r - a2i * a1i,
                a2r * a1i + a2i * a1r,
                a2r * b1r - a2i * b1i + b2r,
                a2r * b1i + a2i * b1r + b2i)

    _, _, s_re, s_im = lax.associative_scan(combine, (a_re, a_im, bu_re, bu_im), axis=1)
    y = (jnp.einsum('blgp,gjp->blgj', s_re, c_re.astype(jnp.float32))
         - jnp.einsum('blgp,gjp->blgj', s_im, c_im.astype(jnp.float32)))
    y = y.reshape(bsz, L, D_SSM) + d.astype(jnp.float32) * uf
    y = jax.nn.gelu(y, approximate=True)
    y = y * jax.nn.sigmoid(y @ glu_w.astype(jnp.float32) + glu_b.astype(jnp.float32))
    return y.astype(u.dtype)


def pool_branch(v, pool_w, pool_b, pool_scale):
    bsz, L, _ = v.shape
    vf = v.astype(jnp.float32).reshape(bsz, L, N_POOL_GROUPS, POOL_GROUP)
    cs = jnp.pad(jnp.cumsum(vf, axis=1), ((0, 0), (1, 0), (0, 0), (0, 0)))
    t = jnp.arange(1, L + 1)
    pooled = []
    for gi, win in enumerate(POOL_WINDOWS):
        start = jnp.maximum(t - win, 0)
        s = cs[:, 1:, gi] - jnp.take(cs[:, :, gi], start, axis=1)
        cnt = (t - start).astype(jnp.float32)
        pooled.append(s / cnt[None, :, None])
    z = jnp.stack(pooled, axis=2) - vf
    z = jnp.einsum('blgc,gcd->blgd', z, pool_w.astype(jnp.float32)) + pool_b.astype(jnp.float32)
    z = z.reshape(bsz, L, D_POOL) * pool_scale.astype(jnp.float32)
    return z.astype(v.dtype)


def causal_depthwise_conv(u, w, b):
    L = u.shape[1]
    up = jnp.pad(u, ((0, 0), (CONV_WIDTH - 1, 0), (0, 0)))
    y = b
    for k in range(CONV_WIDTH):
        y = y + up[:, k:k + L] * w[k]
    return y


def setup_inputs(seed: int = 0) -> dict:
    key = jax.random.key(seed)
    ks = jax.random.split(key, 32)
    f32 = jnp.float32
    nrm = lambda k, shape, s: (jax.random.normal(k, shape, f32) * s).astype(f32)
    gain = lambda k: (1.0 + 0.02 * jax.random.normal(k, (DEPTH, D_MODEL), f32)).astype(f32)
    G, P, GC = N_SSM_GROUPS, SSM_STATE, SSM_GROUP
    n_idx = jnp.arange(P, dtype=f32)
    lam_re = -0.5 + 0.01 * jax.random.normal(ks[3], (DEPTH, G, P), f32)
    lam_im = math.pi * n_idx[None, None, :] + 0.01 * jax.random.normal(ks[4], (DEPTH, G, P), f32)
    log_step = jax.random.uniform(ks[5], (DEPTH, G), f32, math.log(1e-3), math.log(1e-1))
    return {
        "x": nrm(ks[0], (BATCH, SEQ, D_MODEL), 1.0),
        "norm_pre_mix": gain(ks[1]),
        "w_in": nrm(ks[2], (DEPTH, D_MODEL, D_IN), D_MODEL ** -0.5),
        "ssm_lambda_re": lam_re,
        "ssm_lambda_im": lam_im,
        "ssm_log_step": log_step,
        "ssm_b_re": nrm(ks[6], (DEPTH, G, P, GC), (2.0 * GC) ** -0.5),
        "ssm_b_im": nrm(ks[7], (DEPTH, G, P, GC), (2.0 * GC) ** -0.5),
        "ssm_c_re": nrm(ks[8], (DEPTH, G, GC, P), (2.0 * P) ** -0.5),
        "ssm_c_im": nrm(ks[9], (DEPTH, G, GC, P), (2.0 * P) ** -0.5),
        "ssm_d": nrm(ks[10], (DEPTH, D_SSM), 1.0),
        "ssm_glu_w": nrm(ks[11], (DEPTH, D_SSM, D_SSM), D_SSM ** -0.5),
        "ssm_glu_b": nrm(ks[12], (DEPTH, D_SSM), 0.02),
        "pool_w": nrm(ks[13], (DEPTH, N_POOL_GROUPS, POOL_GROUP, POOL_GROUP), POOL_GROUP ** -0.5),
        "pool_b": nrm(ks[14], (DEPTH, N_POOL_GROUPS, POOL_GROUP), 0.02),
        "pool_scale": (1.0 + 0.1 * jax.random.normal(ks[15], (DEPTH, D_POOL), f32)).astype(f32),
        "w_branch_ssm": nrm(ks[16], (DEPTH, D_SSM, D_MODEL), D_SSM ** -0.5),
        "w_branch_pool": nrm(ks[17], (DEPTH, D_POOL, D_MODEL), D_POOL ** -0.5),
        "w_out": nrm(ks[18], (DEPTH, D_MODEL, D_MODEL), D_MODEL ** -0.5),
        "norm_post_mix": gain(ks[19]),
        "norm_pre_ffn": gain(ks[20]),
        "w_up": nrm(ks[21], (DEPTH, D_MODEL, 2 * D_FF), D_MODEL ** -0.5),
        "ffn_conv_w": nrm(ks[22], (DEPTH, CONV_WIDTH, 2 * D_FF), CONV_WIDTH ** -0.5),
        "ffn_conv_b": nrm(ks[23], (DEPTH, 2 * D_FF), 0.02),
        "w_down": nrm(ks[24], (DEPTH, D_FF, D_MODEL), D_FF ** -0.5),
        "norm_post_ffn": gain(ks[25]),
    }


def reference(x, norm_pre_mix, w_in, ssm_lambda_re, ssm_lambda_im, ssm_log_step,
              ssm_b_re, ssm_b_im, ssm_c_re, ssm_c_im, ssm_d, ssm_glu_w, ssm_glu_b,
              pool_w, pool_b, pool_scale, w_branch_ssm, w_branch_pool, w_out,
              norm_post_mix, norm_pre_ffn, w_up, ffn_conv_w, ffn_conv_b, w_down,
              norm_post_ffn):
    h = x
    for i in range(DEPTH):
        a = rms_norm(h, norm_pre_mix[i])
        proj = a @ w_in[i]
        u_ssm = proj[..., :D_SSM]
        u_pool = proj[..., D_SSM:D_SSM + D_POOL]
        g_ssm = proj[..., D_SSM + D_POOL:D_SSM + D_POOL + D_MODEL]
        g_pool = proj[..., D_SSM + D_POOL + D_MODEL:]
        y_ssm = s5_branch(u_ssm, ssm_lambda_re[i], ssm_lambda_im[i], ssm_log_step[i],
                          ssm_b_re[i], ssm_b_im[i], ssm_c_re[i], ssm_c_im[i], ssm_d[i],
                          ssm_glu_w[i], ssm_glu_b[i]) @ w_branch_ssm[i]
        y_pool = pool_branch(u_pool, pool_w[i], pool_b[i], pool_scale[i]) @ w_branch_pool[i]
        merged = jax.nn.sigmoid(g_ssm) * y_ssm + jax.nn.sigmoid(g_pool) * y_pool
        h = h + rms_norm(merged @ w_out[i], norm_post_mix[i])
        c = rms_norm(h, norm_pre_ffn[i])
        up = causal_depthwise_conv(c @ w_up[i], ffn_conv_w[i], ffn_conv_b[i])
        f = jax.nn.gelu(up[..., :D_FF], approximate=True) * up[..., D_FF:]
        h = h + rms_norm(f @ w_down[i], norm_post_ffn[i])
    return h
```

```python
import math
import numpy as np
import concourse.bass as bass
import concourse.mybir as mybir
from concourse.bass_utils import run_bass_kernel_spmd

F32 = mybir.dt.float32
BF16 = mybir.dt.bfloat16
U8 = mybir.dt.uint8
AF = mybir.ActivationFunctionType
ALU = mybir.AluOpType

D = 4096
DS = 2048
NFC = 86
NJ = 2048
JM = 1000
NM = 1048
NPRE = 1000
NH = 524
EPS = 1e-6
TC = 32
NPV = 896
PI = math.pi

ENGS = ("pe", "act", "dve", "pool", "sp")


class _Rec:
    def __init__(self):
        self.call = None

    def __getattr__(self, name):
        def f(*a, **k):
            self.call = (name, a, k)
            return self
        return f


def _bind(fn):
    r = _Rec()
    fn(r)
    name, a, k = r.call
    return lambda e: getattr(e, name)(*a, **k)


class Prog:
    def __init__(self, ndma=28):
        self.ops = {e: [] for e in ENGS}
        self.cnt = {e: 0 for e in ENGS}
        self.res = {}
        self.ndma = ndma
        self.dma_cnt = [0] * ndma
        self.dma_last = [None] * ndma
        self.dma_next = 0

    def _deps(self, reads, writes):
        d = []
        for r in reads:
            st = self.res.get(r)
            if st and st[0]:
                d.append(st[0])
        for w in writes:
            st = self.res.get(w)
            if st:
                if st[0]:
                    d.append(st[0])
                d.extend(st[1].items())
        return d

    def _commit(self, tk, reads, writes):
        for r in reads:
            st = self.res.setdefault(r, [None, {}])
            st[1][tk[0]] = max(st[1].get(tk[0], 0), tk[1])
        for w in writes:
            self.res[w] = [tk, {}]

    def op(self, eng, fn, reads=(), writes=(), signal=True):
        deps = self._deps(reads, writes)
        n = self.cnt[eng] + 1
        tk = (eng, n)
        if signal:
            self.cnt[eng] = n
        self.ops[eng].append((deps, _bind(fn), eng if signal else None, 1))
        self._commit(tk, reads, writes)
        return tk

    def dma(self, eng, fn, reads=(), writes=()):
        deps = self._deps(reads, writes)
        s = self.dma_next
        self.dma_next = (s + 1) % self.ndma
        if self.dma_last[s]:
            deps.append(self.dma_last[s])
        self.dma_cnt[s] += 16
        tk = (("dma", s), self.dma_cnt[s])
        self.dma_last[s] = tk
        self.ops[eng].append((deps, _bind(fn), ("dma", s), 16))
        self._commit(tk, reads, writes)
        return tk

    def barrier(self):
        deps = [(e, self.cnt[e]) for e in ENGS if self.cnt[e] > 0]
        deps += [t for t in self.dma_last if t]
        for e in ENGS:
            self.ops[e].append((list(deps), None, None, 0))
        self.res = {}

    def emit(self, nc, block, sems):
        def semof(k):
            return sems[k]
        hooks = {"pe": block.tensor, "act": block.scalar, "dve": block.vector,
                 "pool": block.gpsimd, "sp": block.sync}
        for eng in ENGS:
            ops = self.ops[eng]

            def body(e, eng=eng, ops=ops):
                seen = {}
                for deps, fn, inc, amt in ops:
                    need = {}
                    for k, v in deps:
                        if k == "pe" and eng == "pe":
                            continue
                        if v > need.get(k, 0):
                            need[k] = v
                    for k, v in need.items():
                        if seen.get(k, 0) >= v:
                            continue
                        seen[k] = v
                        e.wait_ge(semof(k), v)
                    if fn is None:
                        continue
                    ins = fn(e)
                    if inc is not None:
                        ins.then_inc(semof(inc), amt)
            hooks[eng](body)


def _split(n, k):
    base = n // k
    rem = n % k
    out = []
    s = 0
    for i in range(k):
        c = base + (1 if i < rem else 0)
        out.append((s, c))
        s += c
    return out


def build_program(dbg=None):
    nc = bass.Bass("TRN2", target_bir_lowering=False)
    P = Prog()

    def din(name, shape):
        return nc.dram_tensor(name, list(shape), F32, kind="ExternalInput").ap()

    xT = din("xT", [D, NJ])
    pv_d = din("pv", [128, NPV])
    sp3_d = din("sp3", [128, 192])
    invc_d = din("invc", [1, 4 * NM])
    bpad_d = din("bpad", [2, 128, 8192])
    cpad_d = din("cpad", [2, 128, 8192])
    w_in = din("w_in", [D, 12288])
    glu_w = din("glu_w", [DS, DS])
    pool_w = din("pool_w", [2048, 512])
    wbs = din("wbs", [DS, D])
    wbp = din("wbp", [DS, D])
    w_out = din("w_out", [D, D])
    w_up = din("w_up", [D, 2 * NFC * 128])
    w_down = din("w_down", [NFC * 128, D])
    outT = nc.dram_tensor("outT", [D, 1024], F32, kind="ExternalOutput").ap()
    hs = nc.dram_tensor("hs", [D, NM], F32).ap()
    dbg_out = None
    if dbg is not None:
        dbg_out = nc.dram_tensor("dbg", [128, dbg[1]], F32, kind="ExternalOutput").ap()

    ARENA = 206 * 1024
    arena = nc.alloc_sbuf_tensor("arena", [128, ARENA], U8)
    st = {"off": 0}

    def alloc(shape, dt):
        n = int(np.prod(shape[1:])) * (4 if dt == F32 else 2)
        n = (n + 63) // 64 * 64
        off = st["off"]
        assert off + n <= ARENA, ("sbuf overflow", off, n)
        st["off"] = off + n
        v = arena[:, off:off + n].bitcast(dt)
        tot = int(np.prod(shape[1:]))
        v = v[:, 0:tot]
        if len(shape) == 3:
            v = v.rearrange("p (a b) -> p a b", b=shape[2])
        elif len(shape) == 4:
            v = v.rearrange("p (a b c) -> p a b c", b=shape[2], c=shape[3])
        return v

    def mark():
        return st["off"]

    def release(m):
        st["off"] = m

    psb = [nc.alloc_psum_tensor("ps%d" % i, [128, 512], F32) for i in range(8)]
    pstate = {"i": 0, "lim": 8}

    def nextbank():
        i = pstate["i"] % pstate["lim"]
        pstate["i"] = (i + 1) % pstate["lim"]
        return i

    def dump(items, keys):
        st["off"] = wbase
        d0 = alloc([128, dbg[1]], F32)
        P.barrier()
        c = 0
        for ap_, n in items:
            P.op("dve", lambda e, ap_=ap_, c=c, n=n: e.tensor_copy(out=d0[:, c:c + n], in_=ap_), writes=["d0"])
            c += n
        P.dma("sp", lambda e: e.dma_start(out=dbg_out, in_=d0), reads=["d0"])
        return finish(nc, P)

    pv = alloc([128, NPV], F32)
    ones_bf = alloc([128, 128], BF16)
    epsT = alloc([128, 1], F32)
    negpi = alloc([128, 1], F32)
    zst = alloc([128, 2, 64], F32)
    acat = alloc([128, 2, 64], F32)
    aneg = alloc([128, 64], F32)
    apos = alloc([128, 64], F32)
    fre = alloc([128, 64], F32)
    fim = alloc([128, 64], F32)
    rstd_m = alloc([128, NM], F32)
    rstd2 = alloc([128, NM], F32)
    P.dma("sp", lambda e: e.dma_start(out=pv, in_=pv_d), writes=["pv"])
    P.op("dve", lambda e: e.memset(ones_bf, 1.0), writes=["ones"])
    P.op("dve", lambda e: e.memset(epsT, EPS), writes=["eps"])
    P.op("dve", lambda e: e.memset(negpi, PI / 2.0), writes=["negpi"])
    P.op("dve", lambda e: e.memset(zst.rearrange("p a b -> p (a b)"), 0.0), writes=["zst"])

    G1, G2, G3, G4 = 0, 32, 64, 96
    PD, PGB, PPB, PPS = 128, 144, 160, 176
    PCW, PCB, PFLAG = 192, 708, 880

    def pcol(c):
        return pv[:, c:c + 1]

    WSLOT = 8192
    NWS = 2
    wbase = st["off"]
    wslots = [alloc([128, WSLOT], BF16) for _ in range(NWS)]
    wst = {"i": 0}

    def wload(W, k0, KC, c0, ncols):
        assert KC * ncols <= WSLOT
        s = wst["i"]
        wst["i"] = (s + 1) % NWS
        slot = wslots[s][:, 0:KC * ncols].rearrange("p (a b) -> p a b", b=ncols)
        src = W[k0 * 128:(k0 + KC) * 128, c0:c0 + ncols].rearrange("(kc p) n -> p kc n", p=128)
        step = max(1, (2048 // ncols))
        wst["step"] = step
        for a in range(0, KC, step):
            b = min(KC, a + step)
            P.dma("pool", lambda e, a=a, b=b, slot=slot, src=src:
                  e.dma_start(out=slot[:, a:b, :], in_=src[:, a:b, :]),
                  writes=[("w", s)])
        return s, slot

    def fm_layer(W, k0, KC, c0, ncols, gcols, rhs_fn, rhs_keys, blocks, evac):
        for g0 in range(0, ncols, gcols):
            gc = min(gcols, ncols - g0)
            s, slot = wload(W, k0, KC, c0 + g0, gc)
            for m in range(gc // 128):
                banks = [nextbank() for _ in blocks]
                for kc in range(KC):
                    for bi, (b0, bn) in enumerate(blocks):
                        last = (kc == KC - 1)
                        P.op("pe", lambda e, bk=banks[bi], kc=kc, m=m, b0=b0, bn=bn, slot=slot:
                             e.matmul(psb[bk][:, 0:bn], lhsT=slot[:, kc, m * 128:(m + 1) * 128],
                                      rhs=rhs_fn(kc, b0, bn), start=(kc == 0), stop=(kc == KC - 1)),
                             reads=[("w", s)] + list(rhs_keys(kc)), writes=[("ps", banks[bi])],
                             signal=last)
                for bi, (b0, bn) in enumerate(blocks):
                    evac((g0 // 128) + m, bi, banks[bi], b0, bn)

    def norm_stats(src_fn, src_keys, ncols, blocks, rstd_out, rstd_key, scratch):
        banks = [nextbank() for _ in blocks]
        for kc in range(32):
            sq = scratch[kc % 2]
            P.op("act", lambda e, kc=kc, sq=sq: e.activation(out=sq[:, 0:ncols], in_=src_fn(kc), func=AF.Square),
                 reads=list(src_keys(kc)), writes=[("sq", kc % 2)])
            for bi, (b0, bn) in enumerate(blocks):
                P.op("pe", lambda e, bk=banks[bi], b0=b0, bn=bn, sq=sq, kc=kc:
                     e.matmul(psb[bk][:, 0:bn], lhsT=ones_bf, rhs=sq[:, b0:b0 + bn], start=(kc == 0), stop=(kc == 31)),
                     reads=[("sq", kc % 2), "ones"], writes=[("ps", banks[bi])], signal=True)
        for bi, (b0, bn) in enumerate(blocks):
            P.op("act", lambda e, bk=banks[bi], b0=b0, bn=bn:
                 e.activation(out=rstd_out[:, b0:b0 + bn], in_=psb[bk][:, 0:bn], func=AF.Sqrt, bias=epsT, scale=1.0 / D),
                 reads=[("ps", banks[bi]), "eps"], writes=[rstd_key])
        P.op("dve", lambda e: e.reciprocal(out=rstd_out[:, 0:ncols], in_=rstd_out[:, 0:ncols]),
             reads=[rstd_key], writes=[rstd_key])

    blocks_m = _split(NM, 3)
    blocks_p = _split(NPRE, 2)
    blocks_h = _split(NH, 2)

    m_persist = mark()

    sp3 = alloc([128, 192], F32)
    tmp = [alloc([128, 64], F32) for _ in range(8)]
    P.dma("sp", lambda e: e.dma_start(out=sp3, in_=sp3_d), writes=["sp3"])
    lre, lim, lsp = sp3[:, 0:64], sp3[:, 64:128], sp3[:, 128:192]
    dtT, lr, mag, ang, sn, cs, den, t7 = tmp
    P.op("act", lambda e: e.activation(out=dtT, in_=lsp, func=AF.Exp), reads=["sp3"], writes=["dt"])
    P.op("dve", lambda e: e.tensor_scalar_min(out=lr, in0=lre, scalar1=-1e-4), reads=["sp3"], writes=["lr"])
    P.op("dve", lambda e: e.tensor_tensor(out=mag, in0=lr, in1=dtT, op=ALU.mult), reads=["lr", "dt"], writes=["mag"])
    P.op("act", lambda e: e.activation(out=mag, in_=mag, func=AF.Exp), reads=["mag"], writes=["mag"])
    P.op("dve", lambda e: e.tensor_tensor(out=ang, in0=lim, in1=dtT, op=ALU.mult), reads=["sp3", "dt"], writes=["ang"])
    P.op("act", lambda e: e.activation(out=sn, in_=ang, func=AF.Sin, scale=1.0 / 64.0), reads=["ang"], writes=["sn"])
    P.op("act", lambda e: e.activation(out=cs, in_=ang, func=AF.Sin, bias=negpi, scale=1.0 / 64.0), reads=["ang", "negpi"], writes=["cs"])
    for _ in range(6):
        P.op("dve", lambda e: e.tensor_tensor(out=den, in0=sn, in1=cs, op=ALU.mult), reads=["sn", "cs"], writes=["den"])
        P.op("dve", lambda e: e.tensor_tensor(out=cs, in0=cs, in1=cs, op=ALU.mult), reads=["cs", "den"], writes=["cs"])
        P.op("dve", lambda e: e.tensor_tensor(out=sn, in0=sn, in1=sn, op=ALU.mult), reads=["sn", "den"], writes=["sn"])
        P.op("dve", lambda e: e.tensor_tensor(out=cs, in0=cs, in1=sn, op=ALU.subtract), reads=["cs", "sn"], writes=["cs"])
        P.op("dve", lambda e: e.tensor_scalar_mul(out=sn, in0=den, scalar1=2.0), reads=["den", "cs"], writes=["sn"])
    P.op("dve", lambda e: e.tensor_tensor(out=acat[:, 0, :], in0=mag, in1=cs, op=ALU.mult), reads=["mag", "cs"], writes=["ar0"])
    P.op("dve", lambda e: e.tensor_tensor(out=acat[:, 1, :], in0=mag, in1=cs, op=ALU.mult), reads=["mag", "cs"], writes=["ar1"])
    P.op("dve", lambda e: e.tensor_tensor(out=apos, in0=mag, in1=sn, op=ALU.mult), reads=["mag", "sn"], writes=["apos"])
    P.op("dve", lambda e: e.tensor_scalar_mul(out=aneg, in0=apos, scalar1=-1.0), reads=["apos"], writes=["aneg"])
    P.op("dve", lambda e: e.tensor_scalar_add(out=t7, in0=acat[:, 0, :], scalar1=-1.0), reads=["ar0"], writes=["nr"])
    P.op("dve", lambda e: e.tensor_tensor(out=den, in0=lr, in1=lr, op=ALU.mult), reads=["lr"], writes=["den"])
    P.op("dve", lambda e: e.tensor_tensor(out=dtT, in0=lim, in1=lim, op=ALU.mult), reads=["sp3", "ang"], writes=["dt"])
    P.op("dve", lambda e: e.tensor_tensor(out=den, in0=den, in1=dtT, op=ALU.add), reads=["den", "dt"], writes=["den"])
    P.op("dve", lambda e: e.reciprocal(out=den, in_=den), reads=["den"], writes=["den"])
    P.op("dve", lambda e: e.tensor_tensor(out=fre, in0=t7, in1=lr, op=ALU.mult), reads=["nr", "lr"], writes=["fre"])
    P.op("dve", lambda e: e.tensor_tensor(out=dtT, in0=apos, in1=lim, op=ALU.mult), reads=["apos", "sp3", "den"], writes=["dt"])
    P.op("dve", lambda e: e.tensor_tensor(out=fre, in0=fre, in1=dtT, op=ALU.add), reads=["fre", "dt"], writes=["fre"])
    P.op("dve", lambda e: e.tensor_tensor(out=fre, in0=fre, in1=den, op=ALU.mult), reads=["fre", "den"], writes=["fre"])
    P.op("dve", lambda e: e.tensor_tensor(out=fim, in0=apos, in1=lr, op=ALU.mult), reads=["apos", "lr"], writes=["fim"])
    P.op("dve", lambda e: e.tensor_tensor(out=dtT, in0=t7, in1=lim, op=ALU.mult), reads=["nr", "sp3", "fre"], writes=["dt"])
    P.op("dve", lambda e: e.tensor_tensor(out=fim, in0=fim, in1=dtT, op=ALU.subtract), reads=["fim", "dt"], writes=["fim"])
    P.op("dve", lambda e: e.tensor_tensor(out=fim, in0=fim, in1=den, op=ALU.mult), reads=["fim", "den"], writes=["fim"])
    P.barrier()
    release(m_persist)

    uT_m = alloc([128, 16, NM], BF16)
    m_keep = mark()
    uT_pre = alloc([128, 16, NPRE], BF16)
    m_a = mark()

    def make_aT(j0, ncols, blocks, rstd_t, rstd_key, aT, do_stats):
        xs = [alloc([128, ncols], F32) for _ in range(2)] if True else None
        sq = [alloc([128, ncols], BF16) for _ in range(2)]
        if do_stats:
            def src_fn(kc):
                return xs[kc % 2]
            for_keys = lambda kc: [("xs", kc % 2)]
            banks = [nextbank() for _ in blocks]
            for kc in range(32):
                P.dma("sp", lambda e, kc=kc: e.dma_start(out=xs[kc % 2], in_=xT[kc * 128:(kc + 1) * 128, j0:j0 + ncols]),
                      writes=[("xs", kc % 2)])
                P.op("act", lambda e, kc=kc: e.activation(out=sq[kc % 2], in_=xs[kc % 2], func=AF.Square),
                     reads=[("xs", kc % 2)], writes=[("sq", kc % 2)])
                for bi, (b0, bn) in enumerate(blocks):
                    P.op("pe", lambda e, bk=banks[bi], b0=b0, bn=bn, kc=kc:
                         e.matmul(psb[bk][:, 0:bn], lhsT=ones_bf, rhs=sq[kc % 2][:, b0:b0 + bn], start=(kc == 0), stop=(kc == 31)),
                         reads=[("sq", kc % 2), "ones"], writes=[("ps", banks[bi])], signal=True)
            for bi, (b0, bn) in enumerate(blocks):
                P.op("act", lambda e, bk=banks[bi], b0=b0, bn=bn:
                     e.activation(out=rstd_t[:, b0:b0 + bn], in_=psb[bk][:, 0:bn], func=AF.Sqrt, bias=epsT, scale=1.0 / D),
                     reads=[("ps", banks[bi]), "eps"], writes=[rstd_key])
            P.op("dve", lambda e: e.reciprocal(out=rstd_t[:, 0:ncols], in_=rstd_t[:, 0:ncols]), reads=[rstd_key], writes=[rstd_key])
        for kc in range(32):
            P.dma("sp", lambda e, kc=kc: e.dma_start(out=xs[kc % 2], in_=xT[kc * 128:(kc + 1) * 128, j0:j0 + ncols]),
                  writes=[("xs", kc % 2)])
            P.op("dve", lambda e, kc=kc: e.scalar_tensor_tensor(out=aT[:, kc, :], in0=xs[kc % 2], scalar=pcol(G1 + kc),
                                                               in1=rstd_t[:, 0:ncols], op0=ALU.mult, op1=ALU.mult),
                 reads=[("xs", kc % 2), rstd_key, "pv"], writes=[("aT", kc)])

    def evac_copy(dst_fn, key_fn):
        def f(mc, bi, bk, b0, bn):
            P.op("act", lambda e: e.activation(out=dst_fn(mc)[:, b0:b0 + bn], in_=psb[bk][:, 0:bn], func=AF.Copy),
                 reads=[("ps", bk)], writes=[key_fn(mc)])
        return f

    aT_p = alloc([128, 32, NPRE], BF16)
    rstd_p = alloc([128, NPRE], F32)
    m_tmp = mark()
    make_aT(0, NPRE, blocks_p, rstd_p, "rstd_p", aT_p, True)
    release(m_tmp)
    fm_layer(w_in, 0, 32, 0, DS, 256, lambda kc, b0, bn: aT_p[:, kc, b0:b0 + bn], lambda kc: [("aT", kc)],
             blocks_p, evac_copy(lambda mc: uT_pre[:, mc, :], lambda mc: ("uTp", mc)))
    P.barrier()
    release(m_a)
    aT = alloc([128, 32, NM], BF16)
    m_tmp = mark()
    make_aT(JM, NM, blocks_m, rstd_m, "rstd_m", aT, True)
    release(m_tmp)
    fm_layer(w_in, 0, 32, 0, DS, 256, lambda kc, b0, bn: aT[:, kc, b0:b0 + bn], lambda kc: [("aT", kc)],
             blocks_m, evac_copy(lambda mc: uT_m[:, mc, :], lambda mc: ("uTm", mc)))
    P.barrier()
    release(m_a)

    if dbg is not None and dbg[0] == "u":
        d0 = alloc([128, dbg[1]], F32)
        P.op("dve", lambda e: e.tensor_copy(out=d0[:, 0:NM], in_=uT_m[:, 0, :]), reads=[("uTm", 0)], writes=["d0"])
        P.op("dve", lambda e: e.tensor_copy(out=d0[:, NM:2 * NM], in_=uT_m[:, 15, :]), reads=[("uTm", 15)], writes=["d0"])
        P.op("dve", lambda e: e.tensor_copy(out=d0[:, 2 * NM:2 * NM + NPRE], in_=uT_pre[:, 3, :]), reads=[("uTp", 3)], writes=["d0"])
        P.op("dve", lambda e: e.tensor_copy(out=d0[:, 3 * NM:4 * NM], in_=rstd_m), reads=["rstd_m"], writes=["d0"])
        P.dma("sp", lambda e: e.dma_start(out=dbg_out, in_=d0), reads=["d0"])
        return finish(nc, P)

    Bp = alloc([128, 2, 8192], BF16)
    Cp = alloc([128, 2, 8192], BF16)
    for ri in range(2):
        for qq in range(4):
            P.dma("pool", lambda e, ri=ri, qq=qq: e.dma_start(out=Bp[:, ri, qq * 2048:(qq + 1) * 2048],
                                                          in_=bpad_d[ri, :, qq * 2048:(qq + 1) * 2048]),
                  writes=[("Bp", ri, qq)])
    m_ssm = mark()
    cre = alloc([128, 2048], F32)
    cim = alloc([128, 2048], F32)
    ctmp = alloc([128, 128], F32)
    ctmp2 = alloc([128, 128], F32)
    for qq in range(4):
        P.dma("sp", lambda e, qq=qq: e.dma_start(out=cre, in_=cpad_d[0, :, qq * 2048:(qq + 1) * 2048]), writes=["cre"])
        P.dma("sp", lambda e, qq=qq: e.dma_start(out=cim, in_=cpad_d[1, :, qq * 2048:(qq + 1) * 2048]), writes=["cim"])
        for ql in range(16):
            q = qq * 16 + ql
            cs_ = slice(ql * 128, (ql + 1) * 128)
            qs_ = slice(q * 128, (q + 1) * 128)
            P.op("dve", lambda e, q=q, cs_=cs_: e.tensor_scalar_mul(out=ctmp, in0=cim[:, cs_], scalar1=fim[:, q:q + 1]),
                 reads=["cim"], writes=["ctmp"])
            P.op("dve", lambda e, q=q, cs_=cs_, qs_=qs_: e.scalar_tensor_tensor(
                out=Cp[:, 0, qs_], in0=cre[:, cs_], scalar=fre[:, q:q + 1], in1=ctmp, op0=ALU.mult, op1=ALU.subtract),
                reads=["cre", "ctmp"], writes=[("Cp", q)])
            P.op("dve", lambda e, q=q, cs_=cs_: e.tensor_scalar_mul(out=ctmp2, in0=cim[:, cs_], scalar1=fre[:, q:q + 1]),
                 reads=["cim"], writes=["ctmp2"])
            P.op("dve", lambda e, q=q, cs_=cs_, qs_=qs_: e.scalar_tensor_tensor(
                out=Cp[:, 1, qs_], in0=cre[:, cs_], scalar=fim[:, q:q + 1], in1=ctmp2, op0=ALU.mult, op1=ALU.add),
                reads=["cre", "ctmp2"], writes=[("Cp", q)])
    P.barrier()
    release(m_ssm)

    yT = uT_m
    vb = [arena[:, wbase + i * 16384: wbase + (i + 1) * 16384].bitcast(F32).rearrange("p (a b c) -> p a b c", b=64, c=TC)
          for i in range(2)]
    sbf = [alloc([128, 2, 64, TC], BF16) for _ in range(2)]
    t1 = alloc([128, 2, 64], F32)
    t2 = alloc([128, 2, 64], F32)
    ytmp = alloc([128, 16, TC], F32)
    NCH = NJ // TC
    first_out = (JM // TC)
    vbank = [0, 1, 2, 3]
    ybank = [4, 5]
    for n in range(NCH):
        j0 = n * TC
        cur = vb[n % 2]
        prv = vb[(n + 1) % 2]
        for b8 in range(8):
            bk = vbank[(n * 8 + b8) % 4]
            for ql in range(8):
                q = b8 * 8 + ql
                m = q // 4
                if j0 + TC <= NPRE:
                    rhs = uT_pre[:, m, j0:j0 + TC]
                    rk = ("uTp", m)
                elif j0 >= JM:
                    rhs = uT_m[:, m, j0 - JM:j0 - JM + TC]
                    rk = ("uTm", m)
                else:
                    rhs = None
                for ri in range(2):
                    col = (ri * 8 + ql) * TC
                    lastmm = (ql == 7 and ri == 1)
                    if rhs is not None:
                        P.op("pe", lambda e, bk=bk, col=col, q=q, ri=ri, rhs=rhs:
                             e.matmul(psb[bk][:, col:col + TC], lhsT=Bp[:, ri, q * 128:(q + 1) * 128], rhs=rhs,
                                      start=True, stop=True),
                             reads=[("Bp", ri, q // 16), rk], writes=[("ps", bk)], signal=lastmm)
                    else:
                        npre = NPRE - j0
                        P.op("pe", lambda e, bk=bk, col=col, q=q, ri=ri, m=m, npre=npre, j0=j0:
                             e.matmul(psb[bk][:, col:col + npre], lhsT=Bp[:, ri, q * 128:(q + 1) * 128],
                                      rhs=uT_pre[:, m, j0:j0 + npre], start=True, stop=True),
                             reads=[("Bp", ri, q // 16), ("uTp", m)], writes=[("ps", bk)], signal=False)
                        P.op("pe", lambda e, bk=bk, col=col, q=q, ri=ri, m=m, npre=npre:
                             e.matmul(psb[bk][:, col + npre:col + TC], lhsT=Bp[:, ri, q * 128:(q + 1) * 128],
                                      rhs=uT_m[:, m, 0:TC - npre], start=True, stop=True),
                             reads=[("Bp", ri, q // 16), ("uTm", m)], writes=[("ps", bk)], signal=lastmm)
            P.op("act", lambda e, bk=bk, b8=b8, cur=cur:
                 e.activation(out=cur[:, :, b8 * 8:(b8 + 1) * 8, :],
                              in_=psb[bk][:, 0:512].rearrange("p (r q t) -> p r q t", r=2, q=8), func=AF.Copy),
                 reads=[("ps", bk)], writes=[("vb", n % 2)])
        for t in range(TC):
            if t == 0:
                sprev = zst if n == 0 else prv[:, :, :, TC - 1]
                pk = "zst" if n == 0 else ("vb", (n + 1) % 2)
            else:
                sprev = cur[:, :, :, t - 1]
                pk = ("vb", n % 2)
            P.op("dve", lambda e, sprev=sprev: e.tensor_tensor(out=t1, in0=acat, in1=sprev, op=ALU.mult),
                 reads=[pk, "ar0", "ar1"], writes=["t1"])
            P.op("dve", lambda e, sprev=sprev: e.tensor_tensor(out=t2[:, 0, :], in0=aneg, in1=sprev[:, 1, :], op=ALU.mult),
                 reads=[pk, "aneg"], writes=["t2a"])
            P.op("dve", lambda e, sprev=sprev: e.tensor_tensor(out=t2[:, 1, :], in0=apos, in1=sprev[:, 0, :], op=ALU.mult),
                 reads=[pk, "apos"], writes=["t2b"])
            P.op("dve", lambda e: e.tensor_tensor(out=t1, in0=t1, in1=t2, op=ALU.add),
                 reads=["t1", "t2a", "t2b"], writes=["t1"])
            P.op("dve", lambda e, cur=cur, t=t: e.tensor_tensor(out=cur[:, :, :, t], in0=t1, in1=cur[:, :, :, t], op=ALU.add),
                 reads=["t1", ("vb", n % 2)], writes=[("vb", n % 2)])
        if n < first_out:
            continue
        sb_ = sbf[n % 2]
        P.op("act", lambda e, sb_=sb_, cur=cur: e.activation(out=sb_[:, 0, :, :], in_=cur[:, 0, :, :], func=AF.Copy),
             reads=[("vb", n % 2)], writes=[("sbf", n % 2)])
        P.op("act", lambda e, sb_=sb_, cur=cur: e.activation(out=sb_[:, 1, :, :], in_=cur[:, 1, :, :], func=AF.Copy, scale=-1.0),
             reads=[("vb", n % 2)], writes=[("sbf", n % 2)])
        yb = ybank[n % 2]
        for m in range(16):
            for k in range(8):
                q = 4 * m + k // 2
                ri = k % 2
                P.op("pe", lambda e, yb=yb, m=m, q=q, ri=ri, sb_=sb_, k=k:
                     e.matmul(psb[yb][:, m * TC:(m + 1) * TC], lhsT=Cp[:, ri, q * 128:(q + 1) * 128],
                              rhs=sb_[:, ri, q, :], start=(k == 0), stop=(k == 7)),
                     reads=[("Cp", q), ("sbf", n % 2)], writes=[("ps", yb)], signal=(m == 15 and k == 7))
        lo = max(j0, JM) - JM
        hi = j0 + TC - JM
        off = max(j0, JM) - j0
        w_ = hi - lo
        for m in range(16):
            P.op("dve", lambda e, m=m, yb=yb, lo=lo, off=off, w_=w_: e.scalar_tensor_tensor(
                out=ytmp[:, m, 0:w_], in0=uT_m[:, m, lo:lo + w_], scalar=pcol(PD + m),
                in1=psb[yb][:, m * TC + off:m * TC + off + w_], op0=ALU.mult, op1=ALU.add),
                reads=[("ps", yb), ("uTm", m), "pv"], writes=["ytmp"])
        P.op("act", lambda e, lo=lo, w_=w_: e.activation(out=yT[:, :, lo:lo + w_], in_=ytmp[:, :, 0:w_], func=AF.Gelu_apprx_tanh),
             reads=["ytmp"], writes=[("uTm", m) for m in range(16)])
    P.barrier()

    if dbg is not None and dbg[0] == "y":
        st["off"] = wbase
        d0 = alloc([128, dbg[1]], F32)
        P.op("dve", lambda e: e.tensor_copy(out=d0[:, 0:NM], in_=yT[:, 0, :]), reads=["yT"], writes=["d0"])
        P.op("dve", lambda e: e.tensor_copy(out=d0[:, NM:2 * NM], in_=yT[:, 15, :]), reads=["yT"], writes=["d0"])
        P.dma("sp", lambda e: e.dma_start(out=dbg_out, in_=d0), reads=["d0"])
        return finish(nc, P)


    release(m_keep)
    vT = alloc([128, 16, NM], BF16)
    m_s2 = mark()
    aT = alloc([128, 32, NM], BF16)
    m_tmp = mark()
    make_aT(JM, NM, blocks_m, rstd_m, "rstd_m", aT, False)
    release(m_tmp)
    fm_layer(w_in, 0, 32, 2048, 2048, 256, lambda kc, b0, bn: aT[:, kc, b0:b0 + bn], lambda kc: [("aT", kc)],
             blocks_m, evac_copy(lambda mc: vT[:, mc, :], lambda mc: ("vT", mc)))
    P.barrier()
    release(m_s2)
    zpT = alloc([128, 16, NM], BF16)
    m_r = mark()
    invc = alloc([128, 4, NM], F32)
    pa = alloc([128, NM], F32)
    pb = alloc([128, NM], F32)
    P.dma("sp", lambda e: e.dma_start(out=invc.rearrange("p a b -> p (a b)"), in_=invc_d.partition_broadcast(128)), writes=["invc"])
    for m in range(16):
        gi = m // 4
        cur_in = vT[:, m, :]
        ck = ("vT", m)
        bufs = [(pa, "pa"), (pb, "pb")]
        for l in range(gi + 1):
            k = 1 << l
            o, ok = bufs[l % 2]
            P.op("dve", lambda e, o=o, ci=cur_in, k=k: e.tensor_copy(out=o[:, 0:k], in_=ci[:, 0:k]), reads=[ck], writes=[ok])
            P.op("dve", lambda e, o=o, ci=cur_in, k=k: e.tensor_tensor(out=o[:, k:NM], in0=ci[:, k:NM], in1=ci[:, 0:NM - k], op=ALU.add),
                 reads=[ck], writes=[ok])
            cur_in, ck = o, ok
        P.op("dve", lambda e, ci=cur_in, gi=gi: e.tensor_tensor(out=ci, in0=ci, in1=invc[:, gi, :], op=ALU.mult), reads=[ck, "invc"], writes=[ck])
        P.op("dve", lambda e, ci=cur_in, m=m: e.tensor_tensor(out=vT[:, m, :], in0=ci, in1=vT[:, m, :], op=ALU.subtract),
             reads=[ck, ("vT", m)], writes=[("vT", m)])
    for gi in range(4):
        def ev_pool(mc, bi, bk, b0, bn, gi=gi):
            ch = gi * 4 + mc
            P.op("dve", lambda e: e.tensor_scalar(out=zpT[:, ch, b0:b0 + bn], in0=psb[bk][:, 0:bn], scalar1=pcol(PPB + ch),
                                                  scalar2=pcol(PPS + ch), op0=ALU.add, op1=ALU.mult),
                 reads=[("ps", bk), "pv"], writes=[("zpT", ch)])
        fm_layer(pool_w, gi * 4, 4, 0, 512, 512, lambda kc, b0, bn, gi=gi: vT[:, gi * 4 + kc, b0:b0 + bn],
                 lambda kc, gi=gi: [("vT", gi * 4 + kc)], blocks_m, ev_pool)
    P.barrier()
    release(m_r)
    if dbg is not None and dbg[0] == "pool":
        return dump([(vT[:, 0, :], NM), (yT[:, 0, :], NM), (zpT[:, 0, :], NM), (yT[:, 15, :], NM)], None)
    y2T = vT
    sgt = alloc([128, 3, 352], F32)
    def ev_glu(mc, bi, bk, b0, bn):
        P.op("act", lambda e: e.activation(out=sgt[:, bi, 0:bn], in_=psb[bk][:, 0:bn], func=AF.Sigmoid, bias=pcol(PGB + mc)),
             reads=[("ps", bk), "pv"], writes=[("sgt", bi)])
        P.op("dve", lambda e: e.tensor_tensor(out=y2T[:, mc, b0:b0 + bn], in0=yT[:, mc, b0:b0 + bn], in1=sgt[:, bi, 0:bn], op=ALU.mult),
             reads=[("sgt", bi), ("uTm", mc)], writes=[("y2T", mc)])
    fm_layer(glu_w, 0, 16, 0, 2048, 512, lambda kc, b0, bn: yT[:, kc, b0:b0 + bn], lambda kc: [("uTm", kc)], blocks_m, ev_glu)
    P.barrier()
    release(m_r)
    if dbg is not None and dbg[0] == "glu":
        return dump([(y2T[:, 0, :], NM), (y2T[:, 15, :], NM), (yT[:, 0, :], NM), (yT[:, 15, :], NM)], None)
    S0off = None
    aTh = uT_m.rearrange("p a b -> p (a b)")[:, 0:32 * NH].rearrange("p (a b) -> p a b", b=NH)
    merged = alloc([128, 32, NH], BF16)
    sg1 = alloc([128, 2, NH], F32)
    m1 = alloc([128, 2, NH], F32)
    sg2 = alloc([128, 2, NH], F32)
    xs2 = [alloc([128, NH], F32) for _ in range(2)]
    och = [alloc([128, NH], F32) for _ in range(2)]
    sq2 = [alloc([128, NH], BF16) for _ in range(2)]
    rstd_o = alloc([128, NH], F32)
    for hb in range(2):
        h0 = hb * NH
        for kc in range(32):
            P.dma("sp", lambda e, kc=kc, h0=h0: e.dma_start(out=xs2[kc % 2], in_=xT[kc * 128:(kc + 1) * 128, JM + h0:JM + h0 + NH]),
                  writes=[("xs2", kc % 2)])
            P.op("dve", lambda e, kc=kc, h0=h0: e.scalar_tensor_tensor(out=aTh[:, kc, :], in0=xs2[kc % 2], scalar=pcol(G1 + kc),
                                                                      in1=rstd_m[:, h0:h0 + NH], op0=ALU.mult, op1=ALU.mult),
                 reads=[("xs2", kc % 2), "pv"], writes=[("aTh", kc)])
        for grp in range(16):
            def ev_sg(dst, key):
                def f(mc, bi, bk, b0, bn):
                    P.op("act", lambda e: e.activation(out=dst[:, mc, b0:b0 + bn], in_=psb[bk][:, 0:bn], func=AF.Sigmoid),
                         reads=[("ps", bk)], writes=[(key, mc)])
                return f
            def ev_m1(mc, bi, bk, b0, bn):
                P.op("dve", lambda e: e.tensor_tensor(out=m1[:, mc, b0:b0 + bn], in0=sg1[:, mc, b0:b0 + bn], in1=psb[bk][:, 0:bn], op=ALU.mult),
                     reads=[("ps", bk), ("sg1", mc)], writes=[("m1", mc)])
            def ev_mg(mc, bi, bk, b0, bn, grp=grp):
                P.op("dve", lambda e: e.tensor_tensor(out=sg2[:, mc, b0:b0 + bn], in0=sg2[:, mc, b0:b0 + bn], in1=psb[bk][:, 0:bn], op=ALU.mult),
                     reads=[("ps", bk), ("sg2", mc)], writes=[("sg2", mc)])
                P.op("dve", lambda e: e.tensor_tensor(out=merged[:, grp * 2 + mc, b0:b0 + bn], in0=sg2[:, mc, b0:b0 + bn],
                                                      in1=m1[:, mc, b0:b0 + bn], op=ALU.add),
                     reads=[("sg2", mc), ("m1", mc)], writes=[("mg", grp * 2 + mc)])
            rA = lambda kc, b0, bn: aTh[:, kc, b0:b0 + bn]
            kA = lambda kc: [("aTh", kc)]
            fm_layer(w_in, 0, 32, 4096 + grp * 256, 256, 256, rA, kA, blocks_h, ev_sg(sg1, "sg1"))
            fm_layer(wbs, 0, 16, grp * 256, 256, 256, lambda kc, b0, bn, h0=h0: y2T[:, kc, h0 + b0:h0 + b0 + bn],
                     lambda kc: [("y2T", kc)], blocks_h, ev_m1)
            fm_layer(w_in, 0, 32, 8192 + grp * 256, 256, 256, rA, kA, blocks_h, ev_sg(sg2, "sg2"))
            fm_layer(wbp, 0, 16, grp * 256, 256, 256, lambda kc, b0, bn, h0=h0: zpT[:, kc, h0 + b0:h0 + b0 + bn],
                     lambda kc: [("zpT", kc)], blocks_h, ev_mg)
        if dbg is not None and dbg[0] == "mrg" and hb == 0:
            return dump([(merged[:, 0, :], NH), (merged[:, 31, :], NH), (aTh[:, 5, :], NH)], None)
        pstate["lim"] = 6
        pstate["i"] = 0
        def ev_o(mc, bi, bk, b0, bn, h0=h0):
            P.op("act", lambda e: e.activation(out=och[mc % 2][:, b0:b0 + bn], in_=psb[bk][:, 0:bn], func=AF.Copy),
                 reads=[("ps", bk)], writes=[("och", mc % 2)])
            P.op("act", lambda e: e.activation(out=sq2[mc % 2][:, b0:b0 + bn], in_=psb[bk][:, 0:bn], func=AF.Square),
                 reads=[("ps", bk)], writes=[("sq2", mc % 2)])
            P.op("pe", lambda e: e.matmul(psb[6 + bi][:, 0:bn], lhsT=ones_bf, rhs=sq2[mc % 2][:, b0:b0 + bn], start=(mc == 0), stop=(mc == 31)),
                 reads=[("sq2", mc % 2), "ones"], writes=[("ps", 6 + bi)], signal=True)
            if bi == len(blocks_h) - 1:
                P.dma("sp", lambda e: e.dma_start(out=hs[mc * 128:(mc + 1) * 128, h0:h0 + NH], in_=och[mc % 2]),
                      reads=[("och", mc % 2)], writes=[("hs", mc)])
        fm_layer(w_out, 0, 32, 0, D, 256, lambda kc, b0, bn: merged[:, kc, b0:b0 + bn], lambda kc: [("mg", kc)], blocks_h, ev_o)
        def rstd_from(bank0, dst, key, blocks, off=0):
            for bi, (b0, bn) in enumerate(blocks):
                P.op("act", lambda e, bi=bi, b0=b0, bn=bn: e.activation(out=dst[:, off + b0:off + b0 + bn], in_=psb[bank0 + bi][:, 0:bn],
                                                                       func=AF.Sqrt, bias=epsT, scale=1.0 / D),
                     reads=[("ps", bank0 + bi), "eps"], writes=[key])
            P.op("dve", lambda e: e.reciprocal(out=dst[:, off:off + NH], in_=dst[:, off:off + NH]), reads=[key], writes=[key])
        rstd_from(6, rstd_o, "rstd_o", blocks_h)
        for kc in range(32):
            P.dma("sp", lambda e, kc=kc, h0=h0: e.dma_start(out=och[kc % 2], in_=hs[kc * 128:(kc + 1) * 128, h0:h0 + NH]),
                  reads=[("hs", kc)], writes=[("och", kc % 2)])
            P.dma("sp", lambda e, kc=kc, h0=h0: e.dma_start(out=xs2[kc % 2], in_=xT[kc * 128:(kc + 1) * 128, JM + h0:JM + h0 + NH]),
                  writes=[("xs2", kc % 2)])
            P.op("dve", lambda e, kc=kc: e.scalar_tensor_tensor(out=och[kc % 2], in0=och[kc % 2], scalar=pcol(G2 + kc), in1=rstd_o,
                                                               op0=ALU.mult, op1=ALU.mult),
                 reads=[("och", kc % 2), "rstd_o", "pv"], writes=[("och", kc % 2)])
            P.op("dve", lambda e, kc=kc: e.tensor_tensor(out=och[kc % 2], in0=och[kc % 2], in1=xs2[kc % 2], op=ALU.add),
                 reads=[("och", kc % 2), ("xs2", kc % 2)], writes=[("och", kc % 2)])
            P.op("act", lambda e, kc=kc: e.activation(out=sq2[kc % 2], in_=och[kc % 2], func=AF.Square),
                 reads=[("och", kc % 2)], writes=[("sq2", kc % 2)])
            for bi, (b0, bn) in enumerate(blocks_h):
                P.op("pe", lambda e, kc=kc, bi=bi, b0=b0, bn=bn: e.matmul(psb[6 + bi][:, 0:bn], lhsT=ones_bf, rhs=sq2[kc % 2][:, b0:b0 + bn],
                                                                          start=(kc == 0), stop=(kc == 31)),
                     reads=[("sq2", kc % 2), "ones"], writes=[("ps", 6 + bi)], signal=True)
            P.dma("sp", lambda e, kc=kc, h0=h0: e.dma_start(out=hs[kc * 128:(kc + 1) * 128, h0:h0 + NH], in_=och[kc % 2]),
                  reads=[("och", kc % 2)], writes=[("hs", kc)])
        rstd_from(6, rstd2, "rstd2", blocks_h, off=h0)
        pstate["lim"] = 8
        P.barrier()
        if dbg is not None and dbg[0] == "h" and hb == 0:
            for kc in (0, 31):
                P.dma("sp", lambda e, kc=kc: e.dma_start(out=och[kc % 2], in_=hs[kc * 128:(kc + 1) * 128, 0:NH]), writes=[("och", kc % 2)])
            return dump([(och[0], NH), (och[1], NH), (rstd_o, NH), (rstd2[:, 0:NH], NH)], None)
    release(m_persist)
    hist = alloc([128, 172, 2], F32)
    fT = alloc([128, NFC, NH], BF16)
    m_f = mark()
    for hb in range(2):
        h0 = hb * NH
        release(m_f)
        cT = alloc([128, 32, NH], BF16)
        hch = [alloc([128, NH], F32) for _ in range(2)]
        Ug2 = [alloc([128, NH + 2], F32) for _ in range(2)]
        Uv2 = [alloc([128, NH + 2], F32) for _ in range(2)]
        cg = alloc([128, NH], F32)
        cv = alloc([128, NH], F32)
        for kc in range(32):
            P.dma("sp", lambda e, kc=kc, h0=h0: e.dma_start(out=hch[kc % 2], in_=hs[kc * 128:(kc + 1) * 128, h0:h0 + NH]),
                  writes=[("hch", kc % 2)])
            P.op("dve", lambda e, kc=kc, h0=h0: e.scalar_tensor_tensor(out=cT[:, kc, :], in0=hch[kc % 2], scalar=pcol(G3 + kc),
                                                                      in1=rstd2[:, h0:h0 + NH], op0=ALU.mult, op1=ALU.mult),
                 reads=[("hch", kc % 2), "pv"], writes=[("cT", kc)])
            if hb == 0:
                P.op("dve", lambda e, kc=kc: e.tensor_scalar_mul(out=cT[:, kc, 0:24], in0=cT[:, kc, 0:24], scalar1=pcol(PFLAG)),
                     reads=[("cT", kc), "pv"], writes=[("cT", kc)])
        if hb == 0:
            for j in range(2):
                P.op("dve", lambda e, j=j: e.memset(Ug2[j][:, 0:2], 0.0), writes=[("Ug", j)])
                P.op("dve", lambda e, j=j: e.memset(Uv2[j][:, 0:2], 0.0), writes=[("Uv", j)])
        rC = lambda kc, b0, bn: cT[:, kc, b0:b0 + bn]
        kC = lambda kc: [("cT", kc)]
        for fp in range(NFC // 2):
            for half_, (Us, uk, base) in enumerate(((Ug2, "Ug", 0), (Uv2, "Uv", NFC))):
                idx0 = base + 2 * fp
                if hb == 1:
                    for j in range(2):
                        P.op("dve", lambda e, U=Us[j], idx=idx0 + j: e.tensor_copy(out=U[:, 0:2], in_=hist[:, idx, :]),
                             reads=["hist"], writes=[(uk, j)])
                def ev_u(mc, bi, bk, b0, bn, Us=Us, uk=uk):
                    P.op("act", lambda e: e.activation(out=Us[mc][:, 2 + b0:2 + b0 + bn], in_=psb[bk][:, 0:bn], func=AF.Copy),
                         reads=[("ps", bk)], writes=[(uk, mc)])
                fm_layer(w_up, 0, 32, idx0 * 128, 256, 256, rC, kC, blocks_h, ev_u)
            for j in range(2):
                fc = 2 * fp + j
                for (U, uk, idx, cdst, ckey) in ((Ug2[j], ("Ug", j), fc, cg, "cg"), (Uv2[j], ("Uv", j), NFC + fc, cv, "cv")):
                    if hb == 0:
                        P.op("dve", lambda e, U=U, idx=idx: e.tensor_copy(out=hist[:, idx, :], in_=U[:, NH:NH + 2]), reads=[uk], writes=["hist"])
                    P.op("dve", lambda e, U=U, idx=idx, cdst=cdst: e.tensor_scalar(out=cdst, in0=U[:, 2:2 + NH], scalar1=pcol(PCW + 2 * 172 + idx),
                                                                                 scalar2=pcol(PCB + idx), op0=ALU.mult, op1=ALU.add),
                         reads=[uk, "pv"], writes=[ckey])
                    P.op("dve", lambda e, U=U, idx=idx, cdst=cdst: e.scalar_tensor_tensor(out=cdst, in0=U[:, 1:1 + NH], scalar=pcol(PCW + 172 + idx),
                                                                                        in1=cdst, op0=ALU.mult, op1=ALU.add),
                         reads=[uk, ckey, "pv"], writes=[ckey])
                    P.op("dve", lambda e, U=U, idx=idx, cdst=cdst: e.scalar_tensor_tensor(out=cdst, in0=U[:, 0:NH], scalar=pcol(PCW + idx),
                                                                                        in1=cdst, op0=ALU.mult, op1=ALU.add),
                         reads=[uk, ckey, "pv"], writes=[ckey])
                P.op("act", lambda e: e.activation(out=cg, in_=cg, func=AF.Gelu_apprx_tanh), reads=["cg"], writes=["cg"])
                P.op("dve", lambda e, fc=fc: e.tensor_tensor(out=fT[:, fc, :], in0=cg, in1=cv, op=ALU.mult), reads=["cg", "cv"], writes=[("fT", fc)])
        P.barrier()
        release(m_f)
        ffT = alloc([128, 32, NH], F32)
        for p8 in range(8):
            for kc0 in range(0, NFC, 16):
                kcs = min(16, NFC - kc0)
                s_, slot = wload(w_down, kc0, kcs, p8 * 512, 512)
                for kc in range(kcs):
                    for ml in range(4):
                        for bi, (b0, bn) in enumerate(blocks_h):
                            kk = kc0 + kc
                            P.op("pe", lambda e, ml=ml, bi=bi, b0=b0, bn=bn, kc=kc, kk=kk, slot=slot:
                                 e.matmul(psb[ml * 2 + bi][:, 0:bn], lhsT=slot[:, kc, ml * 128:(ml + 1) * 128], rhs=fT[:, kk, b0:b0 + bn],
                                          start=(kk == 0), stop=(kk == NFC - 1)),
                                 reads=[("w", s_), ("fT", kk)], writes=[("ps", ml * 2 + bi)], signal=(kc == kcs - 1))
            for ml in range(4):
                for bi, (b0, bn) in enumerate(blocks_h):
                    P.op("act", lambda e, ml=ml, bi=bi, b0=b0, bn=bn, p8=p8: e.activation(out=ffT[:, p8 * 4 + ml, b0:b0 + bn],
                                                                                        in_=psb[ml * 2 + bi][:, 0:bn], func=AF.Copy),
                         reads=[("ps", ml * 2 + bi)], writes=[("ffT", p8 * 4 + ml)])
        P.barrier()
        sv = st["off"]
        st["off"] = wbase
        sqf = [alloc([128, NH], BF16) for _ in range(2)]
        rstd3 = alloc([128, NH], F32)
        hch = [alloc([128, NH], F32) for _ in range(2)]
        st["off"] = sv
        for kc in range(32):
            P.op("act", lambda e, kc=kc: e.activation(out=sqf[kc % 2], in_=ffT[:, kc, :], func=AF.Square), reads=[("ffT", kc)], writes=[("sqf", kc % 2)])
            for bi, (b0, bn) in enumerate(blocks_h):
                P.op("pe", lambda e, kc=kc, bi=bi, b0=b0, bn=bn: e.matmul(psb[bi][:, 0:bn], lhsT=ones_bf, rhs=sqf[kc % 2][:, b0:b0 + bn],
                                                                          start=(kc == 0), stop=(kc == 31)),
                     reads=[("sqf", kc % 2), "ones"], writes=[("ps", bi)], signal=True)
        for bi, (b0, bn) in enumerate(blocks_h):
            P.op("act", lambda e, bi=bi, b0=b0, bn=bn: e.activation(out=rstd3[:, b0:b0 + bn], in_=psb[bi][:, 0:bn], func=AF.Sqrt, bias=epsT, scale=1.0 / D),
                 reads=[("ps", bi), "eps"], writes=["rstd3"])
        P.op("dve", lambda e: e.reciprocal(out=rstd3, in_=rstd3), reads=["rstd3"], writes=["rstd3"])
        lo = 24 if hb == 0 else 0
        oc0 = 0 if hb == 0 else NH - 24
        ncol = NH - lo
        for kc in range(32):
            P.dma("sp", lambda e, kc=kc, h0=h0: e.dma_start(out=hch[kc % 2], in_=hs[kc * 128:(kc + 1) * 128, h0:h0 + NH]), writes=[("hch", kc % 2)])
            P.op("dve", lambda e, kc=kc: e.scalar_tensor_tensor(out=ffT[:, kc, :], in0=ffT[:, kc, :], scalar=pcol(G4 + kc), in1=rstd3,
                                                               op0=ALU.mult, op1=ALU.mult), reads=[("ffT", kc), "rstd3", "pv"], writes=[("ffT", kc)])
            P.op("dve", lambda e, kc=kc: e.tensor_tensor(out=ffT[:, kc, :], in0=ffT[:, kc, :], in1=hch[kc % 2], op=ALU.add),
                 reads=[("ffT", kc), ("hch", kc % 2)], writes=[("ffT", kc)])
            P.dma("sp", lambda e, kc=kc, lo=lo, oc0=oc0, ncol=ncol: e.dma_start(out=outT[kc * 128:(kc + 1) * 128, oc0:oc0 + ncol],
                                                                              in_=ffT[:, kc, lo:lo + ncol]), reads=[("ffT", kc)], writes=[("out", kc)])
        P.barrier()
    return finish(nc, P)


def finish(nc, P):
    P.barrier()
    sem_names = list(ENGS) + [("dma", i) for i in range(P.ndma)]
    import contextlib
    with contextlib.ExitStack() as es:
        sems = {}
        for k in sem_names:
            nm = k if isinstance(k, str) else "dma%d" % k[1]
            sems[k] = es.enter_context(nc.semaphore("s_" + nm))
        block = es.enter_context(nc.Block())
        P.emit(nc, block, sems)
    return nc


def _prep(inp):
    f = np.float32
    x = np.asarray(inp["x"], f)
    def pcols(v, n):
        return np.ascontiguousarray(np.asarray(v, f).reshape(n, 128).T)
    pvb = np.zeros((128, NPV), f)
    pvb[:, 0:32] = pcols(inp["norm_pre_mix"][0], 32)
    pvb[:, 32:64] = pcols(inp["norm_post_mix"][0], 32)
    pvb[:, 64:96] = pcols(inp["norm_pre_ffn"][0], 32)
    pvb[:, 96:128] = pcols(inp["norm_post_ffn"][0], 32)
    pvb[:, 128:144] = pcols(inp["ssm_d"][0], 16)
    pvb[:, 144:160] = pcols(inp["ssm_glu_b"][0], 16)
    pvb[:, 160:176] = pcols(inp["pool_b"][0].reshape(-1), 16)
    pvb[:, 176:192] = pcols(inp["pool_scale"][0], 16)
    cw = np.asarray(inp["ffn_conv_w"][0], f)
    for k in range(3):
        pvb[:, 192 + k * 172:192 + (k + 1) * 172] = pcols(cw[k], 172)
    pvb[:, 708:880] = pcols(inp["ffn_conv_b"][0], 172)
    lam_re = np.asarray(inp["ssm_lambda_re"][0], f)
    lam_im = np.asarray(inp["ssm_lambda_im"][0], f)
    ls = np.asarray(inp["ssm_log_step"][0], f)
    def l3(a):
        return np.ascontiguousarray(a.reshape(64, 2, 64).transpose(1, 2, 0).reshape(128, 64))
    sp3 = np.concatenate([l3(lam_re), l3(lam_im), l3(np.repeat(ls[:, None], 64, axis=1))], axis=1).astype(f)
    bpad = np.zeros((2, 128, 64, 2, 64), f)
    cpad = np.zeros((2, 2, 64, 64, 128), f)
    for ri, (bk, ck) in enumerate((("ssm_b_re", "ssm_c_re"), ("ssm_b_im", "ssm_c_im"))):
        B = np.asarray(inp[bk][0], f).reshape(64, 2, 64, 16)
        C = np.asarray(inp[ck][0], f).reshape(64, 2, 16, 64)
        for q in range(64):
            for mb in range(2):
                c0 = 32 * (q % 4) + 16 * mb
                bpad[ri, c0:c0 + 16, q, mb, :] = B[q, mb].T
                cpad[ri, mb, :, q, c0:c0 + 16] = C[q, mb].T
    bpad = bpad.reshape(2, 128, 8192)
    cpad = cpad.reshape(2, 128, 8192)
    shared = {
        "sp3": sp3, "bpad": bpad, "cpad": cpad,
        "w_in": np.ascontiguousarray(inp["w_in"][0], f), "glu_w": np.ascontiguousarray(inp["ssm_glu_w"][0], f),
        "pool_w": np.ascontiguousarray(np.asarray(inp["pool_w"][0], f).reshape(2048, 512)),
        "wbs": np.ascontiguousarray(inp["w_branch_ssm"][0], f), "wbp": np.ascontiguousarray(inp["w_branch_pool"][0], f),
        "w_out": np.ascontiguousarray(inp["w_out"][0], f), "w_up": np.ascontiguousarray(inp["w_up"][0], f),
        "w_down": np.ascontiguousarray(inp["w_down"][0], f),
    }
    maps = []
    wins = (2, 4, 8, 16)
    for c in range(8):
        b, h = c // 2, c % 2
        xt = np.zeros((D, NJ), f)
        if h == 1:
            xt[:, :] = x[b].T
        else:
            xt[:, 1024:] = x[b, :1024].T
        pvc = pvb.copy()
        pvc[:, 880] = float(h)
        inv = np.zeros((4, NM), f)
        for gi, w in enumerate(wins):
            for la in range(NM):
                pos = JM + la - 1024 + 1024 * h
                cnt = w if pos < 0 else min(pos + 1, w)
                inv[gi, la] = 1.0 / cnt
        m = dict(shared)
        m["xT"] = xt
        m["pv"] = pvc
        m["invc"] = inv.reshape(1, 4 * NM)
        maps.append(m)
    return maps


def kernel(**inputs):
    maps = _prep(inputs)
    nc = build_program()
    res = run_bass_kernel_spmd(nc, maps, core_ids=list(range(8)))
    out = np.zeros((4, 2048, D), np.float32)
    for c in range(8):
        b, h = c // 2, c % 2
        out[b, h * 1024:(h + 1) * 1024, :] = res.results[c]["outT"].T
    return out
```
